# Optimizing a Trainium2 kernel written in Bass

```python
import jax, jax.numpy as jnp
from jax import lax
import numpy as np

D_MODEL = 1024
BATCH = 8
SEQ = 4096
DEPTH = 1

MIX_WIDTH = D_MODEL
RWKV_WIDTH = MIX_WIDTH // 2
RWKV_HEAD_DIM = 64
RWKV_HEADS = RWKV_WIDTH // RWKV_HEAD_DIM
RWKV_DECAY_LORA = 64
RWKV_AAA_LORA = 64
RWKV_GATE_LORA = 128
RWKV_GN_EPS = 64e-5
GLA_WIDTH = MIX_WIDTH - RWKV_WIDTH
GLA_HEADS = 4
GLA_DV = GLA_WIDTH // GLA_HEADS
GLA_DK = GLA_DV // 2
GLA_KEY_WIDTH = GLA_HEADS * GLA_DK
GLA_GATE_LORA = 16
GLA_GATE_TEMP = 16.0
GLA_CHUNK = 64
CONV_WIDTH = 3
PLE_DIM = 256
D_FF = -(-8 * D_MODEL // (3 * 256)) * 256
NORM_EPS = 1e-6

RWKV_SPLITS = (RWKV_WIDTH, 2 * RWKV_WIDTH, 3 * RWKV_WIDTH,
               3 * RWKV_WIDTH + 2 * RWKV_DECAY_LORA,
               3 * RWKV_WIDTH + 2 * RWKV_DECAY_LORA + RWKV_AAA_LORA)
RWKV_IN_WIDTH = RWKV_SPLITS[-1] + RWKV_GATE_LORA
GLA_SPLITS = (2 * GLA_KEY_WIDTH + GLA_WIDTH, 2 * GLA_KEY_WIDTH + 2 * GLA_WIDTH)
GLA_IN_WIDTH = GLA_SPLITS[-1] + 2 * GLA_GATE_LORA
IN_WIDTH = RWKV_IN_WIDTH + GLA_IN_WIDTH

kernel_name = 'hymba_rwkv7_gla_bidir_encoder_layer'


def rms_norm(x, w, eps=NORM_EPS):
    xf = x.astype(jnp.float32)
    y = xf * lax.rsqrt(jnp.mean(xf * xf, axis=-1, keepdims=True) + eps)
    return (y * w.astype(jnp.float32)).astype(x.dtype)


def centred_shift(z):
    zp = jnp.pad(z, ((0, 0), (1, 1), (0, 0)))
    return 0.5 * (zp[:, :-2] + zp[:, 2:])


def centred_dwconv(z, w):
    pad = (w.shape[0] - 1) // 2
    return lax.conv_general_dilated(z, w.astype(z.dtype), window_strides=(1,),
                                    padding=[(pad, pad)],
                                    dimension_numbers=('NWC', 'WIO', 'NWC'),
                                    feature_group_count=z.shape[-1])


def rwkv7_scan(r, decay, k, v, kk, b, reverse):
    B, T, H, N = r.shape
    xs = tuple(jnp.moveaxis(t, 1, 0) for t in (r, decay, k, v, kk, b))

    def step(S, inp):
        r_t, w_t, k_t, v_t, kk_t, b_t = inp
        sa = jnp.einsum('bhij,bhj->bhi', S, -kk_t)
        S = (S * w_t[:, :, None, :] + sa[..., None] * b_t[:, :, None, :]
             + v_t[..., None] * k_t[:, :, None, :])
        return S, jnp.einsum('bhij,bhj->bhi', S, r_t)

    S0 = jnp.zeros((B, H, N, N), r.dtype)
    _, y = lax.scan(step, S0, xs, reverse=reverse)
    return jnp.moveaxis(y, 0, 1)


def rwkv7_mixer(z, w0, w_up, a0, a_up, g_up, k_k, k_a, r_k, ln_w, ln_b, out_dtype):
    B, T, _ = z.shape
    z = z.astype(jnp.float32)
    r, k, v, w_lo, a_lo, g_lo = jnp.split(z, RWKV_SPLITS, axis=-1)
    w_lo = w_lo.reshape(B, T, 2, RWKV_DECAY_LORA)
    w_log = -jax.nn.softplus(-(w0 + jnp.einsum('btdr,drc->btdc', jnp.tanh(w_lo), w_up))) - 0.5
    decay = jnp.exp(-jnp.exp(w_log))
    a = jax.nn.sigmoid(a0 + a_lo @ a_up)
    g = jax.nn.sigmoid(g_lo) @ g_up

    def heads(t):
        return t.reshape(B, T, RWKV_HEADS, RWKV_HEAD_DIM)

    kk = heads(k * k_k)
    kk = kk * lax.rsqrt(jnp.maximum(jnp.sum(kk * kk, axis=-1, keepdims=True), 1e-24))
    k = k * (1.0 + (a - 1.0) * k_a)
    r_h, k_h, v_h, a_h = heads(r), heads(k), heads(v), heads(a)
    b_h = kk * a_h
    y = (rwkv7_scan(r_h, heads(decay[:, :, 0]), k_h, v_h, kk, b_h, False)
         + rwkv7_scan(r_h, heads(decay[:, :, 1]), k_h, v_h, kk, b_h, True))
    mu = jnp.mean(y, axis=-1, keepdims=True)
    var = jnp.mean(jnp.square(y - mu), axis=-1, keepdims=True)
    y = heads(((y - mu) * lax.rsqrt(var + RWKV_GN_EPS)).reshape(B, T, RWKV_WIDTH) * ln_w + ln_b)
    bonus = jnp.sum(r_h * k_h * r_k, axis=-1, keepdims=True) * v_h
    y = (y + bonus).reshape(B, T, RWKV_WIDTH)
    return (y * g).astype(out_dtype)


def gla_chunked(q, k, v, log_a):
    B, H, T, DK = q.shape
    DV = v.shape[-1]
    nc = T // GLA_CHUNK

    def to_chunks(t):
        return t.reshape(B, H, nc, GLA_CHUNK, t.shape[-1]).transpose(2, 0, 1, 3, 4)

    qc, kc, vc = to_chunks(q), to_chunks(k), to_chunks(v)
    bc = jnp.cumsum(to_chunks(log_a), axis=-2)
    mask = jnp.tril(jnp.ones((GLA_CHUNK, GLA_CHUNK), dtype=bool))[:, :, None]

    def step(S, inp):
        q_c, k_c, v_c, b_c = inp
        o_inter = jnp.einsum('bhcd,bhde->bhce', q_c * jnp.exp(b_c), S)
        diff = b_c[:, :, :, None, :] - b_c[:, :, None, :, :]
        dmat = jnp.exp(jnp.where(mask, diff, -jnp.inf))
        scores = jnp.einsum('bhid,bhjd,bhijd->bhij', q_c, k_c, dmat)
        o_intra = jnp.einsum('bhij,bhje->bhie', scores, v_c)
        b_last = b_c[:, :, -1:, :]
        S = (S * jnp.exp(b_last[:, :, 0, :])[..., None]
             + jnp.einsum('bhcd,bhce->bhde', k_c * jnp.exp(b_last - b_c), v_c))
        return S, o_inter + o_intra

    S0 = jnp.zeros((B, H, DK, DV), q.dtype)
    _, o = lax.scan(step, S0, (qc, kc, vc, bc))
    return o.transpose(1, 2, 0, 3, 4).reshape(B, H, T, DV)


def gla_mixer(z, conv_w, a_up, a_b, norm_w, out_dtype):
    B, T, _ = z.shape
    z = z.astype(jnp.float32)
    qkv, g, a_lo = jnp.split(z, GLA_SPLITS, axis=-1)
    qkv = jax.nn.silu(centred_dwconv(qkv, conv_w))
    q, k, v = jnp.split(qkv, (GLA_KEY_WIDTH, 2 * GLA_KEY_WIDTH), axis=-1)
    a_lo = a_lo.reshape(B, T, 2, GLA_GATE_LORA)
    log_a = jax.nn.log_sigmoid(jnp.einsum('btdr,drc->btdc', a_lo, a_up) + a_b) / GLA_GATE_TEMP

    def heads(t, d):
        return t.reshape(B, T, GLA_HEADS, d).transpose(0, 2, 1, 3)

    def flip(t):
        return jnp.flip(t, axis=2)

    q_h = heads(q, GLA_DK) * GLA_DK ** -0.5
    k_h = heads(k, GLA_DK)
    v_h = heads(v, GLA_DV)
    o_f = gla_chunked(q_h, k_h, v_h, heads(log_a[:, :, 0], GLA_DK))
    o_b = flip(gla_chunked(flip(q_h), flip(k_h), flip(v_h), flip(heads(log_a[:, :, 1], GLA_DK))))
    o = (o_f + o_b).transpose(0, 2, 1, 3)
    o = rms_norm(o, norm_w) * jax.nn.silu(g.reshape(B, T, GLA_HEADS, GLA_DV))
    return o.reshape(B, T, GLA_WIDTH).astype(out_dtype)


def setup_inputs(seed: int = 0) -> dict:
    key = jax.random.key(seed)
    ks = jax.random.split(key, 32)
    L = DEPTH
    f32 = jnp.float32

    def nrm(k, shape, scale):
        return jax.random.normal(k, shape, f32) * scale

    def gain(k, shape):
        return 1.0 + 0.05 * jax.random.normal(k, shape, f32)

    return {
        'x': jax.random.normal(ks[0], (BATCH, SEQ, D_MODEL), f32),
        'p': jax.random.normal(ks[1], (DEPTH, BATCH, SEQ, PLE_DIM), f32),
        'norm_mix_pre': gain(ks[2], (L, D_MODEL)),
        'norm_mix_post': gain(ks[3], (L, D_MODEL)),
        'norm_ffn_pre': gain(ks[4], (L, D_MODEL)),
        'norm_ffn_post': gain(ks[5], (L, D_MODEL)),
        'norm_ple': gain(ks[6], (L, D_MODEL)),
        'w_in': nrm(ks[7], (L, D_MODEL, IN_WIDTH), D_MODEL ** -0.5),
        'rwkv_mu': jax.random.uniform(ks[8], (L, RWKV_IN_WIDTH), f32),
        'rwkv_w0': jax.random.uniform(ks[9], (L, 2, RWKV_WIDTH), f32, -6.0, -1.0),
        'rwkv_w_up': nrm(ks[10], (L, 2, RWKV_DECAY_LORA, RWKV_WIDTH), RWKV_DECAY_LORA ** -0.5),
        'rwkv_a0': nrm(ks[11], (L, RWKV_WIDTH), 0.3),
        'rwkv_a_up': nrm(ks[12], (L, RWKV_AAA_LORA, RWKV_WIDTH), 0.5 * RWKV_AAA_LORA ** -0.5),
        'rwkv_g_up': nrm(ks[13], (L, RWKV_GATE_LORA, RWKV_WIDTH), RWKV_GATE_LORA ** -0.5),
        'rwkv_k_k': 0.85 + 0.05 * jax.random.normal(ks[14], (L, RWKV_WIDTH), f32),
        'rwkv_k_a': gain(ks[15], (L, RWKV_WIDTH)),
        'rwkv_r_k': nrm(ks[16], (L, RWKV_HEADS, RWKV_HEAD_DIM), 0.1),
        'rwkv_ln_w': gain(ks[17], (L, RWKV_WIDTH)),
        'rwkv_ln_b': nrm(ks[18], (L, RWKV_WIDTH), 0.02),
        'gla_conv': nrm(ks[19], (L, CONV_WIDTH, 1, 2 * GLA_KEY_WIDTH + GLA_WIDTH), CONV_WIDTH ** -0.5),
        'gla_a_up': nrm(ks[20], (L, 2, GLA_GATE_LORA, GLA_KEY_WIDTH), GLA_GATE_LORA ** -0.5),
        'gla_a_b': 2.0 + 0.5 * jax.random.normal(ks[21], (L, 2, GLA_KEY_WIDTH), f32),
        'gla_norm': gain(ks[22], (L, GLA_DV)),
        'w_out': nrm(ks[23], (L, MIX_WIDTH, D_MODEL), MIX_WIDTH ** -0.5),
        'ffn_gate': nrm(ks[24], (L, D_MODEL, D_FF), D_MODEL ** -0.5),
        'ffn_up': nrm(ks[25], (L, D_MODEL, D_FF), D_MODEL ** -0.5),
        'ffn_down': nrm(ks[26], (L, D_FF, D_MODEL), D_FF ** -0.5),
        'ple_proj': nrm(ks[27], (L, PLE_DIM, D_MODEL), PLE_DIM ** -0.5),
        'ple_gate': nrm(ks[28], (L, D_MODEL, D_MODEL), D_MODEL ** -0.5),
        'ple_gate_b': nrm(ks[29], (L, D_MODEL), 0.02),
    }


def reference(x, p, norm_mix_pre, norm_mix_post, norm_ffn_pre, norm_ffn_post, norm_ple,
              w_in, rwkv_mu, rwkv_w0, rwkv_w_up, rwkv_a0, rwkv_a_up, rwkv_g_up,
              rwkv_k_k, rwkv_k_a, rwkv_r_k, rwkv_ln_w, rwkv_ln_b,
              gla_conv, gla_a_up, gla_a_b, gla_norm, w_out,
              ffn_gate, ffn_up, ffn_down, ple_proj, ple_gate, ple_gate_b):
    h = x
    for i in range(DEPTH):
        xn = rms_norm(h, norm_mix_pre[i])
        z = xn @ w_in[i]
        z_rwkv, z_gla = z[..., :RWKV_IN_WIDTH], z[..., RWKV_IN_WIDTH:]
        z_rwkv = z_rwkv + rwkv_mu[i] * (centred_shift(z_rwkv) - z_rwkv)
        y_rwkv = rwkv7_mixer(z_rwkv, rwkv_w0[i], rwkv_w_up[i], rwkv_a0[i], rwkv_a_up[i],
                             rwkv_g_up[i], rwkv_k_k[i], rwkv_k_a[i], rwkv_r_k[i],
                             rwkv_ln_w[i], rwkv_ln_b[i], h.dtype)
        y_gla = gla_mixer(z_gla, gla_conv[i], gla_a_up[i], gla_a_b[i], gla_norm[i], h.dtype)
        y = jnp.concatenate([y_rwkv, y_gla], axis=-1) @ w_out[i]
        h = h + rms_norm(y, norm_mix_post[i])
        hn = rms_norm(h, norm_ffn_pre[i])
        f = (jax.nn.silu(hn @ ffn_gate[i]) * (hn @ ffn_up[i])) @ ffn_down[i]
        h = h + rms_norm(f, norm_ffn_post[i])
        e = p[i] @ ple_proj[i]
        gate = jax.nn.sigmoid(h @ ple_gate[i] + ple_gate_b[i])
        h = h + rms_norm(gate * e, norm_ple[i])
    return h
```

```python
import numpy as np
import concourse.bass as bass
import concourse.mybir as mybir
from concourse.bass_utils import run_bass_kernel_spmd

F32 = mybir.dt.float32
BF16 = mybir.dt.bfloat16
AF = mybir.ActivationFunctionType
ALU = mybir.AluOpType
AX = mybir.AxisListType


class Buf:
    __slots__ = ("name", "w", "r", "rg", "rgop")

    def __init__(self, name=""):
        self.name = name
        self.w = None
        self.r = []
        self.rg = None
        self.rgop = None


class Op:
    __slots__ = ("eng", "fn", "deps", "idx", "sig", "is_dma", "dsem", "dval", "dprev", "forced")

    def __init__(self, eng, fn, is_dma):
        self.eng = eng
        self.fn = fn
        self.deps = []
        self.sig = 0
        self.is_dma = is_dma
        self.dsem = None
        self.dval = 0
        self.dprev = 0
        self.forced = set()


class Prog:
    ENGS = ("pe", "dve", "act", "pool", "sp")
    NDMASEM = 12
    _uid = [0]
    G = None

    @staticmethod
    def init_sems(nc, st):
        G = {'esem': {}, 'ebase': {}, 'dsem': {}, 'dcnt': {}, 'di': {}}
        for e in Prog.ENGS:
            G['esem'][e] = st.enter_context(nc.semaphore('s_' + e))
            G['ebase'][e] = 0
            G['dsem'][e] = [st.enter_context(nc.semaphore('d_%s%d' % (e, i))) for i in range(Prog.NDMASEM)]
            G['dcnt'][e] = [0] * Prog.NDMASEM
            G['di'][e] = 0
        Prog.G = G

    def __init__(self, nc):
        self.nc = nc
        self.ops = {e: [] for e in self.ENGS}
        self.all = []

    def _add(self, eng, fn, reads, writes, is_dma=False):
        op = Op(eng, fn, is_dma)
        deps = {}
        for b in reads:
            if b.w is not None:
                deps[id(b.w)] = b.w
        for b in writes:
            if b.w is not None:
                deps[id(b.w)] = b.w
            for r in b.r:
                deps[id(r)] = r
        deps.pop(id(op), None)
        op.deps = list(deps.values())
        for b in reads:
            b.r.append(op)
        for b in writes:
            b.w = op
            b.r = []
        self.ops[eng].append(op)
        self.all.append(op)
        return op

    def pe(self, fn, reads=(), writes=()):
        return self._add("pe", fn, reads, writes)

    def dve(self, fn, reads=(), writes=()):
        return self._add("dve", fn, reads, writes)

    def act(self, fn, reads=(), writes=()):
        return self._add("act", fn, reads, writes)

    def pool(self, fn, reads=(), writes=()):
        return self._add("pool", fn, reads, writes)

    def dma(self, fn, reads=(), writes=(), q="sp"):
        return self._add(q, fn, reads, writes, is_dma=True)

    def emit(self, final_wait_ops=()):
        nc = self.nc
        need = set()
        for op in self.all:
            for d in op.deps:
                if not d.is_dma:
                    if d.eng == "pe" and op.eng == "pe" and not op.is_dma and id(d) not in op.forced:
                        continue
                    need.add(id(d))
        for e in self.ENGS:
            c = 0
            for op in self.ops[e]:
                if op.is_dma:
                    continue
                if id(op) in need:
                    c += 1
                    op.sig = c
                else:
                    op.sig = -c
        import contextlib
        G = Prog.G
        with contextlib.ExitStack() as st:
            esem = G['esem']
            base = G['ebase']
            for e in self.ENGS:
                nsig = 0
                for op in self.ops[e]:
                    if (not op.is_dma) and op.sig > 0:
                        op.sig += base[e]
                        nsig += 1
                base[e] += nsig
            for q in self.ENGS:
                sems = G['dsem'][q]
                cnt = G['dcnt'][q]
                i = G['di'][q]
                for op in self.ops[q]:
                    if not op.is_dma:
                        continue
                    j = i % len(sems)
                    op.dsem = sems[j]
                    op.dprev = cnt[j]
                    cnt[j] += 16
                    op.dval = cnt[j]
                    i += 1
                G['di'][q] = i
            block = st.enter_context(nc.Block())
            handles = {"pe": nc.tensor, "dve": nc.vector, "act": nc.scalar,
                       "pool": nc.gpsimd, "sp": nc.sync}

            def run_engine(e, eng):
                waited = {}

                def wait(sem, key, val):
                    if waited.get(key, 0) >= val:
                        return
                    waited[key] = val
                    eng.wait_ge(sem, val)

                for op in self.ops[e]:
                    for d in op.deps:
                        if d.is_dma:
                            wait(d.dsem, ("d", id(d.dsem)), d.dval)
                        else:
                            if d.eng == "pe" and e == "pe" and not op.is_dma and id(d) not in op.forced:
                                continue
                            wait(esem[d.eng], ("e", d.eng), d.sig)
                    if op.is_dma:
                        if op.dprev:
                            wait(op.dsem, ("d", id(op.dsem)), op.dprev)
                        ins = op.fn(eng)
                        ins.then_inc(op.dsem, 16)
                    else:
                        ins = op.fn(eng)
                        if op.sig > 0:
                            ins.then_inc(esem[e], 1)
                last = {}
                for op in self.ops[e]:
                    if op.is_dma:
                        last[id(op.dsem)] = op
                for d in last.values():
                    wait(d.dsem, ("d", id(d.dsem)), d.dval)

            @block.tensor
            def _(eng):
                run_engine("pe", eng)

            @block.vector
            def _(eng):
                run_engine("dve", eng)

            @block.scalar
            def _(eng):
                run_engine("act", eng)

            @block.gpsimd
            def _(eng):
                run_engine("pool", eng)

            @block.sync
            def _(eng):
                run_engine("sp", eng)

import contextlib

CDEC = float(np.exp(-0.5))
NG = 28
MU, KK_, KA_, A0_, RK_, W0_, CV_, AB_, NW_, NF_ = 0, 15, 19, 23, 27, 31, 39, 63, 67, 75
NPV = 83
R_LNW, R_LNB, R_GN, R_NMP, R_NFP, R_NPL, R_PGB = 0, 512, 1024, 1536, 2560, 3584, 4608
NROW = 5632
C_ID, C_OB = 0, 128
C_MA = [256, 896]; C_MB = [512, 1152]; C_MZ = [768, 1408]
C_MG = [1536, 1664]
NCST = 1792


def host_prep(I):
    f = np.float32
    w_in = I['w_in'][0]
    G0 = 1856
    cols = []
    z = lambda n: np.zeros((1024, n), f)
    for hp in range(4): cols.append(w_in[:, hp * 128:(hp + 1) * 128])
    for hp in range(4): cols.append(w_in[:, 512 + hp * 128:512 + (hp + 1) * 128])
    for hp in range(4): cols.append(w_in[:, 1024 + hp * 128:1024 + (hp + 1) * 128])
    cols.append(w_in[:, 1536:1664])
    cols.append(np.concatenate([w_in[:, 1664:1728], z(64)], 1))
    cols.append(w_in[:, 1728:1856])
    for kp in range(2): cols.append(w_in[:, G0 + kp * 128:G0 + (kp + 1) * 128])
    for kp in range(2): cols.append(w_in[:, G0 + 256 + kp * 128:G0 + 256 + (kp + 1) * 128])
    for h in range(4): cols.append(w_in[:, G0 + 512 + h * 128:G0 + 512 + (h + 1) * 128])
    for h in range(4): cols.append(w_in[:, G0 + 1024 + h * 128:G0 + 1024 + (h + 1) * 128])
    cols.append(np.concatenate([w_in[:, G0 + 1536:G0 + 1552], z(16), w_in[:, G0 + 1552:G0 + 1568], z(80)], 1))
    w_in_p = np.ascontiguousarray(np.concatenate(cols, 1))
    pv = np.zeros((128, NPV), f)
    mu = I['rwkv_mu'][0]
    for gi in range(13): pv[:, MU + gi] = mu[gi * 128:(gi + 1) * 128]
    pv[:64, MU + 13] = mu[1664:1728]
    pv[:, MU + 14] = mu[1728:1856]
    for hp in range(4):
        s = slice(hp * 128, (hp + 1) * 128)
        pv[:, KK_ + hp] = I['rwkv_k_k'][0][s]
        pv[:, KA_ + hp] = I['rwkv_k_a'][0][s]
        pv[:, A0_ + hp] = I['rwkv_a0'][0][s]
        pv[:, RK_ + hp] = I['rwkv_r_k'][0].reshape(512)[s]
        for d in range(2): pv[:, W0_ + d * 4 + hp] = I['rwkv_w0'][0][d][s]
    cw = I['gla_conv'][0][:, 0, :]
    for g in range(8):
        for tap in range(3): pv[:, CV_ + g * 3 + tap] = cw[tap, g * 128:(g + 1) * 128]
    for d in range(2):
        for kp in range(2): pv[:, AB_ + d * 2 + kp] = I['gla_a_b'][0][d][kp * 128:(kp + 1) * 128]
    pv[:, NW_:NW_ + 8] = I['norm_mix_pre'][0].reshape(8, 128).T
    pv[:, NF_:NF_ + 8] = I['norm_ffn_pre'][0].reshape(8, 128).T
    rows = np.zeros((128, NROW), f)
    rows[:, R_LNW:R_LNW + 512] = I['rwkv_ln_w'][0][None]
    rows[:, R_LNB:R_LNB + 512] = I['rwkv_ln_b'][0][None]
    rows[:, R_GN:R_GN + 512] = np.tile(I['gla_norm'][0], 4)[None]
    rows[:, R_NMP:R_NMP + 1024] = I['norm_mix_post'][0][None]
    rows[:, R_NFP:R_NFP + 1024] = I['norm_ffn_post'][0][None]
    rows[:, R_NPL:R_NPL + 1024] = I['norm_ple'][0][None]
    rows[:, R_PGB:R_PGB + 1024] = I['ple_gate_b'][0][None]
    lora = np.zeros((128, 1792), f)
    lora[:, 0:512] = I['rwkv_w_up'][0].reshape(128, 512)
    lora[:64, 512:1024] = I['rwkv_a_up'][0]
    lora[:, 1024:1536] = I['rwkv_g_up'][0]
    lora[0:16, 1536:1792] = I['gla_a_up'][0][0]
    lora[32:48, 1536:1792] = I['gla_a_up'][0][1]
    return dict(w_in_p=w_in_p, pvec=pv, rows=rows, lora=lora,
                w_out=np.ascontiguousarray(I['w_out'][0]), ffn_gate=np.ascontiguousarray(I['ffn_gate'][0]),
                ffn_up=np.ascontiguousarray(I['ffn_up'][0]), ffn_down=np.ascontiguousarray(I['ffn_down'][0]),
                ple_proj=np.ascontiguousarray(I['ple_proj'][0]), ple_gate=np.ascontiguousarray(I['ple_gate'][0]))


def make_consts(TB):
    f = np.float32
    c = np.zeros((128, NCST), f)
    i = np.arange(128)
    c[:, C_ID:C_ID + 128] = np.eye(128)
    c[:, C_OB:C_OB + 128] = (i[:, None] // 64 == i[None, :] // 64)
    same = (i[:, None] // 64 == i[None, :] // 64)
    for d in range(2):
        pf_lt = (i[:, None] < i[None, :]) if d == 0 else (i[:, None] > i[None, :])
        pf_le = pf_lt | np.eye(128, dtype=bool)
        strictT = (pf_lt & same).astype(f)
        inclT = (pf_le & same).astype(f)
        strict = strictT.T
        c[:, C_MA[d]:C_MA[d] + 256] = np.concatenate([-inclT, -strictT], 1)
        c[:, C_MB[d]:C_MB[d] + 256] = np.concatenate([inclT, strictT], 1)
        c[:, C_MZ[d]:C_MZ[d] + 128] = -strict
        c[:, C_MG[d]:C_MG[d] + 128] = pf_le.astype(f)
    t = np.arange(TB)
    resets = np.zeros((128, 2 * TB), f)
    resets[:, 0:TB] = (t % 64 != 0)[None]
    resets[:, TB:2 * TB] = (t % 128 != 0)[None]
    return c, resets


class TT_:
    def __init__(self, t, name):
        self.t = t
        self.b = Buf(name)

    def __getitem__(self, k):
        return self.t[k]


class KB:
    def __init__(self, nc, st):
        self.nc = nc
        self.st = st
        self.P = Prog(nc)
        self.n = 0

    _uid = [0]

    def sb(self, shape, dt=F32, name=None):
        KB._uid[0] += 1
        name = name or ("t%d" % KB._uid[0])
        return TT_(self.st.enter_context(self.nc.sbuf_tensor(name, shape, dt)), name)

    def ps(self, name):
        KB._uid[0] += 1
        name = "%s_%d" % (name, KB._uid[0])
        return TT_(self.st.enter_context(self.nc.psum_tensor(name, [128, 512], F32)), name)

    @staticmethod
    def _bs(l):
        return [x.b if isinstance(x, TT_) else x for x in l]

    def A(self, out, in_, func, r, w, **kw):
        return self.P.act(lambda e: e.activation(out=out, in_=in_, func=func, **kw), self._bs(r), self._bs(w))

    def TT(self, eng, out, in0, in1, op, r, w):
        return self.P._add(eng, lambda e: e.tensor_tensor(out=out, in0=in0, in1=in1, op=op), self._bs(r), self._bs(w))

    def TS(self, eng, out, in0, s1, s2, op0, op1, r, w):
        if op1 is None:
            return self.P._add(eng, lambda e: e.tensor_scalar(out=out, in0=in0, scalar1=s1, scalar2=None, op0=op0), self._bs(r), self._bs(w))
        return self.P._add(eng, lambda e: e.tensor_scalar(out=out, in0=in0, scalar1=s1, scalar2=s2, op0=op0, op1=op1), self._bs(r), self._bs(w))

    def STT(self, out, in0, scalar, in1, op0, op1, r, w):
        return self.P.dve(lambda e: e.scalar_tensor_tensor(out=out, in0=in0, scalar=scalar, in1=in1, op0=op0, op1=op1), self._bs(r), self._bs(w))

    def CP(self, eng, out, in_, r, w):
        if eng == "act":
            return self.P.act(lambda e: e.copy(out=out, in_=in_), self._bs(r), self._bs(w))
        return self.P._add(eng, lambda e: e.tensor_copy(out=out, in_=in_), self._bs(r), self._bs(w))

    def MM(self, out, lhsT, rhs, start, stop, r, w):
        op = self.P.pe(lambda e: e.matmul(out, lhsT=lhsT, rhs=rhs, start=start, stop=stop), self._bs(r), self._bs(w))
        psz = lhsT.partition_size()
        rg = None if psz > 64 else (lhsT.start_partition(), psz)
        for b in self._bs(w):
            if rg is not None and b.rg is not None and b.rg != rg and b.rgop is not None:
                if all(id(b.rgop) != id(d) for d in op.deps):
                    op.deps.append(b.rgop)
                op.forced.add(id(b.rgop))
            b.rg = rg
            b.rgop = op
        return op

    def TR(self, out, in_, ident, r, w):
        return self.P.pe(lambda e: e.transpose(out, in_, ident), self._bs(r), self._bs(w))

    def DMA(self, out, in_, r, w, q="sp"):
        return self.P.dma(lambda e: e.dma_start(out=out, in_=in_), self._bs(r), self._bs(w), q=q)

    def SCAN(self, out, d0, d1, r, w):
        return self.P.dve(lambda e: e.tensor_tensor_scan(out=out, data0=d0, data1=d1, initial=0.0, op0=ALU.mult, op1=ALU.add), self._bs(r), self._bs(w))

    def RECIP(self, out, in_, r, w):
        return self.P.dve(lambda e: e.reciprocal(out=out, in_=in_), self._bs(r), self._bs(w))

    def RSUM(self, out, in_, r, w):
        return self.P.dve(lambda e: e.tensor_reduce(out=out, in_=in_, axis=AX.X, op=ALU.add), self._bs(r), self._bs(w))


def chunks512(n):
    out = []
    c = 0
    while c < n:
        m = min(512, n - c)
        out.append((c, m))
        c += m
    return out


def _reset_bufs(ts):
    for t in ts:
        t.b.w = None
        t.b.r = []
        t.b.rg = None
        t.b.rgop = None


def phase1(nc, T, D):
    NT = T // 128
    TB = min(512, T)
    NBLK = T // TB
    TPB = TB // 128
    with contextlib.ExitStack() as st0:
      K0 = KB(nc, st0)
      pv = K0.sb([128, NPV]); cst = K0.sb([128, 256]); rst = K0.sb([128, 2 * TB])
      lorab = K0.sb([128, 1792], BF16)
      cstb = K0.sb([128, 128], BF16)
      der = K0.sb([128, 64])
      xT = K0.sb([128, 8, T + 2], BF16)
      outer = [pv, cst, rst, lorab, cstb, der, xT]
      with contextlib.ExitStack() as st:
        K = KB(nc, st)
        loraf = K.sb([128, 1792])
        banks = [K.ps("pb%d" % i) for i in range(2)]
        K.DMA(pv[:], D['pvec'][:, :], [], [pv])
        K.DMA(cst[:], D['consts'][:, 0:256], [], [cst])
        K.DMA(rst[:], D['resets'][:, :], [], [rst])
        K.DMA(loraf[:], D['lora'][:, :], [], [loraf])
        K.CP("dve", lorab[:], loraf[:], [loraf], [lorab])
        K.CP("dve", cstb[:], cst[:, C_ID:C_ID + 128], [cst], [cstb])
        K.TS("dve", der[:, 0:15], pv[:, MU:MU + 15], -1.0, 1.0, ALU.mult, ALU.add, [pv], [der])
        K.TS("dve", der[:, 15:30], pv[:, MU:MU + 15], 0.5, None, ALU.mult, None, [pv], [der])
        K.TS("dve", der[:, 30:34], pv[:, KA_:KA_ + 4], -1.0, 1.0, ALU.mult, ALU.add, [pv], [der])
        K.TS("dve", der[:, 34:38], pv[:, AB_:AB_ + 4], -1.0, None, ALU.mult, None, [pv], [der])
        K.P.pool(lambda e: e.memset(xT[:, :, 0:1], 0.0), [], [xT.b])
        K.P.pool(lambda e: e.memset(xT[:, :, T + 1:T + 2], 0.0), [], [xT.b])
        xts = [K.sb([128, 1024]) for _ in range(2)]
        xnb = [K.sb([128, 1024], BF16) for _ in range(2)]
        junk = K.sb([128, 1024], BF16)
        st1 = [K.sb([128, 4]) for _ in range(2)]
        for i in range(NT):
            s = i % 2
            K.DMA(xts[s][:], D['x'][i * 128:(i + 1) * 128, :], [], [xts[s]], q="sp" if s == 0 else "pool")
            K.A(junk[:], xts[s][:], AF.Square, [xts[s]], [junk, st1[s]], accum_out=st1[s][:, 0:1])
            K.A(st1[s][:, 1:2], st1[s][:, 0:1], AF.Sqrt, [st1[s]], [st1[s]], scale=1.0 / 1024, bias=1e-6)
            K.RECIP(st1[s][:, 2:3], st1[s][:, 1:2], [st1[s]], [st1[s]])
            K.TS("dve", xnb[s][:], xts[s][:], st1[s][:, 2:3], None, ALU.mult, None, [xts[s], st1[s]], [xnb[s]])
            pb = banks[i % 2]
            pbv = pb[:].bitcast(BF16).rearrange("p (a b) -> p a b", a=8)
            for kc in range(8):
                K.TR(pbv[:, kc, :], xnb[s][:, kc * 128:(kc + 1) * 128], cstb[:], [xnb[s], cstb], [pb])
            K.CP("act" if i % 2 else "dve", xT[:, :, 1 + i * 128:1 + (i + 1) * 128], pbv, [pb], [xT])
        K.P.emit()
      _reset_bufs(outer)
      with contextlib.ExitStack() as st:
        K = KB(nc, st)
        banks = [None, None] + [K.ps("pb%d" % i) for i in range(6)]
        wf = [K.sb([128, 8, 128]) for _ in range(2)]
        wb = [K.sb([128, 8, 128], BF16) for _ in range(2)]
        ZW = TB + 2
        zc = [K.sb([128, ZW]) for _ in range(3)]
        nbuf = 14
        B = [K.sb([128, TB]) for _ in range(nbuf)]
        twT = K.sb([128, TB], BF16); aloT = K.sb([128, TB], BF16); sgT = K.sb([128, TB], BF16); galoT = K.sb([128, TB], BF16)
        vb = K.sb([128, TB], BF16)
        ORK = [K.sb([128, 2, TB], BF16) for _ in range(2)]
        OKB = [K.sb([128, 2, TB], BF16) for _ in range(2)]
        tmb = [K.sb([128, TPB, 128], BF16) for _ in range(2)]
        tmf = [K.sb([128, TPB, 128]) for _ in range(1)]
        pcb = [K.sb([128, TB // 64]) for _ in range(2)]
        cnt = {'w': 0, 'bank': 0, 'z': 0, 'o': 0, 'tm': 0}
        nw_b = pv[:, NW_:NW_ + 8].unsqueeze(2).to_broadcast([128, 8, 128])

        def nbank():
            cnt['bank'] += 1
            return banks[2 + cnt['bank'] % 6]

        def proj(gi, t0, halo, evac):
            s = cnt['w'] % 2
            cnt['w'] += 1
            K.DMA(wf[s][:], D['w_in_p'][:, gi * 128:(gi + 1) * 128].rearrange("(kc p) m -> p kc m", p=128), [], [wf[s]],
                  q="sp" if s == 0 else "pool")
            K.TT("pool", wb[s][:], wf[s][:], nw_b, ALU.mult, [wf[s], pv], [wb[s]])
            rng = chunks512(TB + 2) if halo else [(c + 1, n) for (c, n) in chunks512(TB)]
            for (c, n) in rng:
                bk = nbank()
                for kc in range(8):
                    K.MM(bk[:, 0:n], wb[s][:, kc, :], xT[:, kc, t0 + c:t0 + c + n], kc == 0, kc == 7, [wb[s], xT], [bk])
                evac(bk, c if halo else c - 1, n)

        def proj_z(gi, t0):
            z = zc[cnt['z'] % 3]
            cnt['z'] += 1
            i = [0]

            def ev(bk, c, n):
                K.CP("act" if i[0] % 2 == 0 else "dve", z[:, c:c + n], bk[:, 0:n], [bk], [z])
                i[0] += 1
            proj(gi, t0, True, ev)
            return z

        def shift(dst, z, gi, t1, t2):
            K.TT("pool", t1[:], z[:, 0:TB], z[:, 2:TB + 2], ALU.add, [z], [t1])
            K.A(t2[:], z[:, 1:TB + 1], AF.Copy, [z, der], [t2], scale=der[:, gi:gi + 1])
            K.STT(dst[:], t1[:], der[:, 15 + gi:16 + gi], t2[:], ALU.mult, ALU.add, [t1, t2, der], [dst])

        def lora_mm(dst, lhsT, rhs_t, rhs_rows, func, r, **kw):
            for (c, n) in chunks512(TB):
                bk = nbank()
                K.MM(bk[:, 0:n], lhsT, rhs_t[rhs_rows, c:c + n], True, True, [lorab, rhs_t], [bk])
                K.A(dst[:, c:c + n], bk[:, 0:n], func, [bk] + r, [dst], **kw)

        def ones_mm(src, fn):
            for (c, n) in chunks512(TB):
                bk = nbank()
                K.MM(bk[:, 0:n], cst[:, C_OB:C_OB + 128], src[:, c:c + n], True, True, [cst, src], [bk])
                fn(bk, c, n)

        def to_tm(src, bf, dram, col0, t0):
            s = cnt['tm'] % 2
            cnt['tm'] += 1
            dst = tmb[s] if bf else tmf[0]
            idn = cstb if bf else cst
            per = 8 if bf else 4
            for g0 in range(0, TPB, per):
                bk = nbank()
                gn = min(per, TPB - g0)
                bv = (bk[:].bitcast(BF16) if bf else bk[:]).rearrange("p (a b) -> p a b", b=128)
                for j in range(gn):
                    K.TR(bv[:, j, :], src[:, (g0 + j) * 128:(g0 + j + 1) * 128], idn[:, C_ID:C_ID + 128] if not bf else idn[:], [src, idn], [bk])
                K.CP("act", dst[:, g0:g0 + gn, :], bv[:, 0:gn, :], [bk], [dst])
            K.DMA(dram[t0:t0 + TB, col0:col0 + 128].rearrange("(i p) f -> p i f", p=128), dst[:], [dst], [], q="pool")

        def decay_products(S, s_, t0, d, chunk, cdec, pcs, pcdst, prods):
            nch = TB // chunk
            tE, tX = B[12], B[13]
            if d == 1:
                Sv = S[:].rearrange("p (c k) -> p c k", k=chunk)
                K.STT(tX[:], S[:], -1.0, s_[:], ALU.mult, ALU.add, [S, s_], [tX])
                K.TT("pool", tE[:].rearrange("p (c k) -> p c k", k=chunk), tX[:].rearrange("p (c k) -> p c k", k=chunk),
                     Sv[:, :, chunk - 1:chunk].to_broadcast([128, nch, chunk]), ALU.add, [tX, S], [tE])
                K.CP("pool", S[:], tE[:], [tE], [S])
            last = 0 if d == 1 else chunk - 1
            Sv = S[:].rearrange("p (c k) -> p c k", k=chunk)
            K.A(pcs[:, 0:nch], Sv[:, :, last], AF.Exp, [S], [pcs], scale=-cdec)
            K.DMA(pcdst, pcs[:, 0:nch], [pcs], [], q="pool")
            K.A(tE[:], S[:], AF.Exp, [S], [tE], scale=-cdec)
            for (kind, src, sc, o) in prods:
                if kind == 'neg':
                    if sc is None:
                        K.TT("pool", o, src[:], tE[:], ALU.mult, [src, tE], [o_b[0]])
                    else:
                        K.STT(o, src[:], sc, tE[:], ALU.mult, ALU.mult, [src, tE], [o_b[0]])
            if any(k == 'negx' for (k, _, _, _) in prods):
                K.TT("pool", tX[:], S[:], s_[:], ALU.subtract, [S, s_], [tX])
                K.A(tX[:], tX[:], AF.Exp, [tX], [tX], scale=-cdec)
                for (kind, src, sc, o) in prods:
                    if kind == 'negx':
                        K.TT("dve", o, src[:], tX[:], ALU.mult, [src, tX], [o_b[0]])
            K.A(tE[:], S[:], AF.Exp, [S], [tE], scale=cdec)
            for (kind, src, sc, o) in prods:
                if kind == 'pos':
                    K.TT("dve" if o_b[2] else "pool", o, src[:], tE[:], ALU.mult, [src, tE], [o_b[1]])
                    o_b[2] = not o_b[2]

        o_b = [None, None, True]

        for blk in range(NBLK):
            t0 = blk * TB
            ti0 = blk * TPB
            z = proj_z(12, t0); shift(B[0], z, 12, B[1], B[2]); K.A(twT[:], B[0][:], AF.Tanh, [B[0]], [twT])
            z = proj_z(13, t0); shift(B[0], z, 13, B[1], B[2]); K.CP("pool", aloT[:], B[0][:], [B[0]], [aloT])
            z = proj_z(14, t0); shift(B[0], z, 14, B[1], B[2]); K.A(sgT[:], B[0][:], AF.Sigmoid, [B[0]], [sgT])
            K.DMA(D['SG'][:, t0:t0 + TB], sgT[:], [sgT], [], q="pool")
            for hp in range(4):
                Br, Bk, Bv, Ba, Bkk, Bt, Bt2, Bs, BS = B[3], B[4], B[5], B[6], B[7], B[8], B[9], B[10], B[11]
                z = proj_z(hp, t0); shift(Br, z, hp, B[1], B[2])
                z = proj_z(4 + hp, t0); shift(Bk, z, 4 + hp, B[1], B[2])
                z = proj_z(8 + hp, t0); shift(Bv, z, 8 + hp, B[1], B[2])
                K.CP("pool", vb[:], Bv[:], [Bv], [vb])
                to_tm(vb, True, D['V_tm'], hp * 128, t0)
                lora_mm(Ba, lorab[0:64, 512 + hp * 128:512 + (hp + 1) * 128], aloT, slice(0, 64), AF.Sigmoid, [pv],
                        bias=pv[:, A0_ + hp:A0_ + hp + 1])
                K.TS("pool", Bkk[:], Bk[:], pv[:, KK_ + hp:KK_ + hp + 1], None, ALU.mult, None, [Bk, pv], [Bkk])
                K.TT("pool", Bt[:], Bkk[:], Bkk[:], ALU.mult, [Bkk], [Bt])

                def kkfn(bk, c, n):
                    K.TS("dve", Bt2[:, c:c + n], bk[:, 0:n], 1e-24, None, ALU.max, None, [bk], [Bt2])
                ones_mm(Bt, kkfn)
                K.A(Bt2[:], Bt2[:], AF.Sqrt, [Bt2], [Bt2])
                K.RECIP(Bt2[:], Bt2[:], [Bt2], [Bt2])
                K.TT("dve", Bkk[:], Bkk[:], Bt2[:], ALU.mult, [Bkk, Bt2], [Bkk])
                K.TS("dve", Bt[:], Ba[:], pv[:, KA_ + hp:KA_ + hp + 1], der[:, 30 + hp:31 + hp], ALU.mult, ALU.add, [Ba, pv, der], [Bt])
                K.TT("pool", Bk[:], Bk[:], Bt[:], ALU.mult, [Bk, Bt], [Bk])
                K.TT("pool", Ba[:], Bkk[:], Ba[:], ALU.mult, [Bkk, Ba], [Ba])
                K.STT(Bt[:], Br[:], pv[:, RK_ + hp:RK_ + hp + 1], Bk[:], ALU.mult, ALU.mult, [Br, Bk, pv], [Bt])

                def bvfn(bk, c, n):
                    K.TT("dve", Bt2[:, c:c + n], bk[:, 0:n], Bv[:, c:c + n], ALU.mult, [bk, Bv], [Bt2])
                ones_mm(Bt, bvfn)
                to_tm(Bt2, False, D['BV_tm'], hp * 128, t0)
                for d in range(2):
                    lora_mm(Bs, lorab[d * 64:(d + 1) * 64, hp * 128:(hp + 1) * 128], twT, slice(d * 64, (d + 1) * 64), AF.Sigmoid, [pv],
                            bias=pv[:, W0_ + d * 4 + hp:W0_ + d * 4 + hp + 1])
                    K.SCAN(BS[:], rst[:, 0:TB], Bs[:], [rst, Bs], [BS])
                    so = cnt['o'] % 2
                    cnt['o'] += 1
                    o_b[0], o_b[1] = ORK[so].b, OKB[so].b
                    nch = TB // 64
                    decay_products(BS, Bs, t0, d, 64, CDEC, pcb[so], D['PC%d' % d][:, hp, blk * nch:(blk + 1) * nch],
                                   [('neg', Br, None, ORK[so][:, 0, :]), ('negx', Bkk, None, ORK[so][:, 1, :]),
                                    ('pos', Bk, None, OKB[so][:, 0, :]), ('pos', Ba, None, OKB[so][:, 1, :])])
                    for j in range(2):
                        K.DMA(D['RK%d' % d][ti0:ti0 + TPB, :, hp, j, :].rearrange("i p t -> p i t"),
                              ORK[so][:, j, :].rearrange("p (i t) -> p i t", t=128), [ORK[so]], [])
                        K.DMA(D['KB%d' % d][ti0:ti0 + TPB, :, hp, j, :].rearrange("i p t -> p i t"),
                              OKB[so][:, j, :].rearrange("p (i t) -> p i t", t=128), [OKB[so]], [], q="pool")
            ev_i = [0]

            def ev_galo(bk, c, n):
                K.CP("dve", galoT[:, c:c + n], bk[:, 0:n], [bk], [galoT])
            proj(27, t0, False, ev_galo)

            def conv(dst, z, g, t1):
                cw = lambda tap: pv[:, CV_ + g * 3 + tap:CV_ + g * 3 + tap + 1]
                K.A(t1[:], z[:, 0:TB], AF.Copy, [z, pv], [t1], scale=cw(0))
                K.STT(t1[:], z[:, 1:TB + 1], cw(1), t1[:], ALU.mult, ALU.add, [z, t1, pv], [t1])
                K.STT(t1[:], z[:, 2:TB + 2], cw(2), t1[:], ALU.mult, ALU.add, [z, t1, pv], [t1])
                K.A(dst[:], t1[:], AF.Silu, [t1], [dst])
            Bq = [B[3], B[4]]; Bkg = [B[5], B[6]]
            for kp in range(2):
                z = proj_z(15 + kp, t0); conv(Bq[kp], z, kp, B[1])
                z = proj_z(17 + kp, t0); conv(Bkg[kp], z, 2 + kp, B[1])
            for h in range(4):
                z = proj_z(19 + h, t0); conv(vb, z, 4 + h, B[1])
                to_tm(vb, True, D['VG_tm'], h * 128, t0)
            for h in range(4):
                def ev_gg(bk, c, n):
                    K.A(B[7][:, c:c + n], bk[:, 0:n], AF.Silu, [bk], [B[7]])
                proj(23 + h, t0, False, ev_gg)
                to_tm(B[7], False, D['GG_tm'], h * 128, t0)
            for d in range(2):
                for kp in range(2):
                    Bs, BS = B[10], B[11]
                    lora_mm(Bs, lorab[d * 32:d * 32 + 16, 1536 + kp * 128:1536 + (kp + 1) * 128], galoT, slice(d * 32, d * 32 + 16),
                            AF.Exp, [der], scale=-1.0, bias=der[:, 34 + d * 2 + kp:35 + d * 2 + kp])
                    K.A(Bs[:], Bs[:], AF.Ln, [Bs], [Bs], bias=1.0)
                    K.SCAN(BS[:], rst[:, TB:2 * TB], Bs[:], [rst, Bs], [BS])
                    so = cnt['o'] % 2
                    cnt['o'] += 1
                    o_b[0], o_b[1] = ORK[so].b, ORK[so].b
                    nch = TB // 128
                    decay_products(BS, Bs, t0, d, 128, 1.0 / 16, pcb[so], D['PCG%d' % d][:, kp, blk * nch:(blk + 1) * nch],
                                   [('neg', Bq[kp], 0.125, ORK[so][:, 0, :]), ('pos', Bkg[kp], None, ORK[so][:, 1, :])])
                    for j in range(2):
                        K.DMA(D['QK%d' % d][ti0:ti0 + TPB, :, kp, j, :].rearrange("i p t -> p i t"),
                              ORK[so][:, j, :].rearrange("p (i t) -> p i t", t=128), [ORK[so]], [], q="sp" if j else "pool")
        K.P.emit()


def phase2_rwkv(nc, T, D, d):
    NT = T // 128
    NC64 = T // 64
    with contextlib.ExitStack() as st:
        K = KB(nc, st)
        cst = K.sb([128, NCST]); cstb = K.sb([128, 128], BF16)
        PCt = K.sb([128, 4, NC64])
        pb = [K.ps("pb%d" % i) for i in range(8)]
        K.DMA(cst[:], D['consts'][:, :], [], [cst])
        K.DMA(PCt[:], D['PC%d' % d][:, :, :], [], [PCt])
        K.CP("dve", cstb[:], cst[:, C_ID:C_ID + 128], [cst], [cstb])
        RKt = [K.sb([128, 4, 2, 128], BF16) for _ in range(2)]
        KBt = [K.sb([128, 4, 2, 128], BF16) for _ in range(2)]
        Vt = [K.sb([128, 512], BF16) for _ in range(2)]
        RZ1 = [K.sb([128, 4, 128], BF16) for _ in range(2)]
        RZ2 = [K.sb([128, 4, 128], BF16) for _ in range(2)]
        TM = K.sb([128, 3, 4, 128], BF16)
        SA = [K.sb([128, 4, 256], BF16) for _ in range(2)]
        SB = [K.sb([128, 4, 256], BF16) for _ in range(2)]
        SZq = [K.sb([128, 4, 128], BF16) for _ in range(2)]
        Nq = [[K.sb([128, 4, 128], BF16) for _ in range(2)] for _ in range(2)]
        Zq = [[K.sb([128, 4, 128], BF16) for _ in range(2)] for _ in range(2)]
        Xq = [[K.sb([128, 4, 128], BF16) for _ in range(2)] for _ in range(2)]
        WT = K.sb([128, 4, 128], BF16)
        AVq = [K.sb([128, 4, 64], BF16) for _ in range(2)]
        UT = K.sb([128, 8, 64])
        H32 = K.sb([128, 4, 64]); Ht = K.sb([128, 4, 64])
        Hbf = [K.sb([128, 4, 64], BF16) for _ in range(4)]
        Ub = K.sb([128, 8, 64], BF16)
        Yt = [K.sb([128, 512]) for _ in range(2)]
        for t_ in RZ1 + RZ2:
            K.P.pool(lambda e, t_=t_: e.memset(t_[:], 0.0), [], [t_.b])
        K.P.pool(lambda e: e.memset(H32[:], 0.0), [], [H32.b])
        K.P.pool(lambda e: e.memset(Hbf[0][:], 0.0), [], [Hbf[0].b])
        mA = cst[:, C_MA[d]:C_MA[d] + 256].unsqueeze(1).to_broadcast([128, 2, 256])
        mB = cst[:, C_MB[d]:C_MB[d] + 256].unsqueeze(1).to_broadcast([128, 2, 256])
        mZ = cst[:, C_MZ[d]:C_MZ[d] + 128].unsqueeze(1).to_broadcast([128, 4, 128])
        idb = cst[:, C_ID:C_ID + 128].unsqueeze(1).to_broadcast([128, 4, 128])
        v4 = lambda bk: bk[:].rearrange("p (a b) -> p a b", a=4)
        v8 = lambda bk: bk[:].rearrange("p (a b) -> p a b", a=8)
        tiles = list(range(NT)) if d == 0 else list(range(NT - 1, -1, -1))
        order = [(0, 64), (64, 128)] if d == 0 else [(64, 128), (0, 64)]
        cur = 0
        for it, i in enumerate(tiles):
            s = it % 2
            K.DMA(RKt[s][:], D['RK%d' % d][i], [], [RKt[s]])
            K.DMA(KBt[s][:], D['KB%d' % d][i], [], [KBt[s]], q="pool")
            K.DMA(Vt[s][:], D['V_tm'][i * 128:(i + 1) * 128, :], [], [Vt[s]])
            K.CP("pool", RZ1[s][:, :, 0:64], RKt[s][:, :, 0, 0:64], [RKt[s]], [RZ1[s]])
            K.CP("pool", RZ2[s][:, :, 64:128], RKt[s][:, :, 0, 64:128], [RKt[s]], [RZ2[s]])
            t0v = pb[0][:].bitcast(BF16).rearrange("p (a b) -> p a b", a=8)
            t1v = pb[1][:].bitcast(BF16).rearrange("p (a b) -> p a b", a=8)
            for hp in range(4):
                K.TR(t0v[:, hp, :], RKt[s][:, hp, 1, :], cstb[:], [RKt[s], cstb], [pb[0]])
                K.TR(t0v[:, 4 + hp, :], KBt[s][:, hp, 0, :], cstb[:], [KBt[s], cstb], [pb[0]])
                K.TR(t1v[:, hp, :], KBt[s][:, hp, 1, :], cstb[:], [KBt[s], cstb], [pb[1]])
            K.CP("act", TM[:, 0:2, :, :].rearrange("p a b c -> p (a b) c"), t0v, [pb[0]], [TM])
            K.TS("dve", TM[:, 2, :, :], t1v[:, 0:4, :], -1.0, None, ALU.mult, None, [pb[1]], [TM])
            hds = [[(2 * q + hh // 2, hh % 2) for hh in range(4)] for q in range(2)]
            for q in range(2):
                hd = hds[q]
                for hh, (hp, par) in enumerate(hd):
                    rs = slice(par * 64, (par + 1) * 64)
                    rk = RKt[s][rs, hp, :, :].rearrange("p j t -> p (j t)")
                    K.MM(pb[hh // 2][:, (hh % 2) * 256:(hh % 2 + 1) * 256], KBt[s][rs, hp, 1, :], rk, True, True, [KBt[s], RKt[s]], [pb[hh // 2]])
                    K.MM(pb[2 + hh // 2][:, (hh % 2) * 256:(hh % 2 + 1) * 256], KBt[s][rs, hp, 0, :], rk, True, True, [KBt[s], RKt[s]], [pb[2 + hh // 2]])
                    K.MM(v4(pb[4])[:, hh, :], RKt[s][rs, hp, 1, :], KBt[s][rs, hp, 1, :], True, True, [KBt[s], RKt[s]], [pb[4]])
                for b2 in range(2):
                    K.TT("dve", SA[q][:, 2 * b2:2 * b2 + 2, :], pb[b2][:].rearrange("p (a b) -> p a b", a=2), mA, ALU.mult, [pb[b2], cst], [SA[q]])
                    K.TT("dve", SB[q][:, 2 * b2:2 * b2 + 2, :], pb[2 + b2][:].rearrange("p (a b) -> p a b", a=2), mB, ALU.mult, [pb[2 + b2], cst], [SB[q]])
                K.TT("dve", SZq[q][:], v4(pb[4]), mZ, ALU.mult, [pb[4], cst], [SZq[q]])
                K.TT("pool", Xq[q][0][:], SA[q][:, :, 128:256], idb, ALU.add, [SA[q], cst], [Xq[q][0]])
            nbk = [(pb[4], pb[5], pb[6]), (pb[1], pb[2], pb[3])]
            Ncur = [(lambda hh, q=q: SA[q][:, hh, 128:256]) for q in range(2)]
            Zcur = [(lambda hh, q=q: SZq[q][:, hh, :]) for q in range(2)]
            nbuf_ = [SA[0], SA[1]]
            zbuf_ = [SZq[0], SZq[1]]
            for k in range(1, 6):
                for q in range(2):
                    bN, bZ, bX = nbk[q]
                    for hh in range(4):
                        if k < 5:
                            K.MM(v4(bN)[:, hh, :], Zcur[q](hh), Ncur[q](hh), True, True, [nbuf_[q], zbuf_[q]], [bN])
                        K.MM(v4(bZ)[:, hh, :], Ncur[q](hh), Zcur[q](hh), True, True, [nbuf_[q], zbuf_[q]], [bZ])
                for q in range(2):
                    bN, bZ, bX = nbk[q]
                    if k < 5:
                        K.CP("act", Nq[q][k % 2][:], v4(bN), [bN], [Nq[q][k % 2]])
                    K.CP("dve" if q == 0 else "act", Zq[q][k % 2][:], v4(bZ), [bZ], [Zq[q][k % 2]])
                for q in range(2):
                    bN, bZ, bX = nbk[q]
                    for hh in range(4):
                        K.MM(v4(bX)[:, hh, :], Zq[q][k % 2][:, hh, :], Xq[q][(k - 1) % 2][:, hh, :], True, True, [Zq[q][k % 2], Xq[q][(k - 1) % 2]], [bX])
                for q in range(2):
                    bN, bZ, bX = nbk[q]
                    K.TT("dve", Xq[q][k % 2][:], Xq[q][(k - 1) % 2][:], v4(bX), ALU.add, [Xq[q][(k - 1) % 2], bX], [Xq[q][k % 2]])
                    if k < 5:
                        Ncur[q] = (lambda hh, q=q, k=k: Nq[q][k % 2][:, hh, :])
                        nbuf_[q] = Nq[q][k % 2]
                    Zcur[q] = (lambda hh, q=q, k=k: Zq[q][k % 2][:, hh, :])
                    zbuf_[q] = Zq[q][k % 2]
            for q in range(2):
                hd = hds[q]
                X = Xq[q][1]
                pW = pb[0] if q == 0 else pb[4]
                pA = pb[7] if q == 0 else pb[5]
                for hh, (hp, par) in enumerate(hd):
                    K.MM(v4(pW)[:, hh, :], TM[:, 0, hp, :], X[:, hh, :], True, True, [TM, X], [pW])
                wv = pW[:].rearrange("p (a b c) -> p a b c", a=2, b=2)
                for par in range(2):
                    rs = slice(par * 64, (par + 1) * 64)
                    K.CP("act", WT[rs, 2 * q:2 * q + 2, :], wv[rs, :, par, :], [pW], [WT])
                av = pA[:, 0:256].rearrange("p (a b) -> p a b", a=4)
                uv = pA[:, 256:512].rearrange("p (a b) -> p a b", a=4)
                for hh, (hp, par) in enumerate(hd):
                    h = 2 * hp + par
                    K.MM(av[:, hh, :], SB[q][:, hh, 128:256], Vt[s][:, h * 64:(h + 1) * 64], True, True, [SB[q], Vt[s]], [pA])
                K.CP("act" if q == 0 else "dve", AVq[q][:], av, [pA], [AVq[q]])
                for hh in range(4):
                    K.MM(uv[:, hh, :], X[:, hh, :], AVq[q][:, hh, :], True, True, [X, AVq[q]], [pA])
                K.CP("act" if q == 0 else "dve", UT[:, 4 * q:4 * q + 4, :], uv, [pA], [UT])
            hidx = []
            for (c0, c1) in order:
                cs = slice(c0, c1)
                hidx.append(cur)
                nxt = (cur + 1) % 4
                for h in range(8):
                    hp, par = h // 2, h % 2
                    rs = slice(par * 64, (par + 1) * 64)
                    K.MM(v8(pb[7])[:, h, :], WT[rs, hp, :], Hbf[cur][rs, hp, :], True, True, [WT, Hbf[cur]], [pb[7]])
                K.TT("dve", Ub[cs, :, :], v8(pb[7])[cs, :, :], UT[cs, :, :], ALU.add, [pb[7], UT], [Ub])
                for hp in range(4):
                    K.MM(v4(pb[2])[:, hp, :], TM[cs, 1, hp, :], Vt[s][cs, hp * 128:(hp + 1) * 128], True, False, [TM, Vt[s]], [pb[2]])
                    K.MM(v4(pb[2])[:, hp, :], TM[cs, 2, hp, :], Ub[cs, 2 * hp:2 * hp + 2, :].rearrange("p a b -> p (a b)"), False, True, [TM, Ub], [pb[2]])
                ch = i * 2 + c0 // 64
                for par in range(2):
                    rs = slice(par * 64, (par + 1) * 64)
                    pcb_ = PCt[rs, :, ch:ch + 1].to_broadcast([64, 4, 64])
                    K.TT("dve", Ht[rs, :, :], H32[rs, :, :], v4(pb[2])[rs, :, par * 64:(par + 1) * 64], ALU.add, [H32, pb[2]], [Ht])
                    K.TT("dve", H32[rs, :, :], Ht[rs, :, :], pcb_, ALU.mult, [Ht, PCt], [H32])
                    K.TT("dve", Hbf[nxt][rs, :, :], Ht[rs, :, :], pcb_, ALU.mult, [Ht, PCt], [Hbf[nxt]])
                cur = nxt
            rzs = [RZ1[s], RZ2[s]] if d == 0 else [RZ2[s], RZ1[s]]
            for h in range(8):
                hp, par, q, hh = h // 2, h % 2, h // 4, h % 4
                rs = slice(par * 64, (par + 1) * 64)
                o = v8(pb[3])[:, h, :]
                K.MM(o, SB[q][:, hh, 0:128], Vt[s][:, h * 64:(h + 1) * 64], True, False, [SB[q], Vt[s]], [pb[3]])
                K.MM(o, rzs[0][rs, hp, :], Hbf[hidx[0]][rs, hp, :], False, False, [rzs[0], Hbf[hidx[0]]], [pb[3]])
                K.MM(o, rzs[1][rs, hp, :], Hbf[hidx[1]][rs, hp, :], False, False, [rzs[1], Hbf[hidx[1]]], [pb[3]])
                K.MM(o, SA[q][:, hh, 0:128], Ub[:, h, :], False, True, [SA[q], Ub], [pb[3]])
            K.CP("act", Yt[s][:], pb[3][:], [pb[3]], [Yt[s]])
            K.DMA(D['Y%d' % d][i * 128:(i + 1) * 128, :], Yt[s][:], [Yt[s]], [], q="pool")
        K.P.emit()


import os
KCUT = int(os.environ.get('KCUT', '9'))


def phase2_gla(nc, T, D, d):
    NT = T // 128
    with contextlib.ExitStack() as st:
        K = KB(nc, st)
        cst = K.sb([128, NCST]); cstb = K.sb([128, 128], BF16)
        PCt = K.sb([128, 2, NT])
        pb = [K.ps("pb%d" % i) for i in range(8)]
        K.DMA(cst[:], D['consts'][:, :], [], [cst])
        K.DMA(PCt[:], D['PCG%d' % d][:, :, :], [], [PCt])
        K.CP("dve", cstb[:], cst[:, C_ID:C_ID + 128], [cst], [cstb])
        QKt = [K.sb([128, 2, 2, 128], BF16) for _ in range(2)]
        Vt = [K.sb([128, 512], BF16) for _ in range(2)]
        KTM = [K.sb([128, 2, 128], BF16) for _ in range(2)]
        ST = [K.sb([128, 4, 128], BF16) for _ in range(2)]
        H32 = K.sb([128, 2, 128]); Ht = K.sb([128, 2, 128])
        Hbf = [K.sb([128, 2, 128], BF16) for _ in range(2)]
        Ot = [K.sb([128, 512]) for _ in range(2)]
        K.P.pool(lambda e: e.memset(H32[:], 0.0), [], [H32.b])
        K.P.pool(lambda e: e.memset(Hbf[0][:], 0.0), [], [Hbf[0].b])
        mG = cst[:, C_MG[d]:C_MG[d] + 128].unsqueeze(1).to_broadcast([128, 4, 128])
        v4 = lambda bk: bk[:].rearrange("p (a b) -> p a b", a=4)
        tiles = list(range(NT)) if d == 0 else list(range(NT - 1, -1, -1))
        cur = 0
        for it, i in enumerate(tiles):
            s = it % 2
            pS, pO, pG, pT = pb[(it % 2) * 4], pb[(it % 2) * 4 + 1], pb[(it % 2) * 4 + 2], pb[(it % 2) * 4 + 3]
            K.DMA(QKt[s][:], D['QK%d' % d][i], [], [QKt[s]])
            K.DMA(Vt[s][:], D['VG_tm'][i * 128:(i + 1) * 128, :], [], [Vt[s]], q="pool")
            if KCUT <= 1:
                continue
            tv = pT[:].bitcast(BF16).rearrange("p (a b) -> p a b", a=8)
            for kp in range(2):
                K.TR(tv[:, kp, :], QKt[s][:, kp, 1, :], cstb[:], [QKt[s], cstb], [pT])
            K.CP("act", KTM[s][:], tv[:, 0:2, :], [pT], [KTM[s]])
            if KCUT <= 2:
                continue
            for h in range(4):
                kp, par = h // 2, h % 2
                rs = slice(par * 64, (par + 1) * 64)
                K.MM(v4(pS)[:, h, :], QKt[s][rs, kp, 1, :], QKt[s][rs, kp, 0, :], True, True, [QKt[s]], [pS])
            if os.environ.get('KVAR') == 'a':
                for h in range(4):
                    K.TT("dve", ST[s][:, h, :], v4(pS)[:, h, :], cst[:, C_MG[d]:C_MG[d] + 128], ALU.mult, [pS, cst], [ST[s]])
            elif os.environ.get('KVAR') == 'c':
                pass
            elif os.environ.get('KVAR') == 'b':
                K.CP("dve", ST[s][:], v4(pS), [pS], [ST[s]])
            else:
                K.TT("dve", ST[s][:], v4(pS), mG, ALU.mult, [pS, cst], [ST[s]])
            if KCUT <= 3:
                continue
            for h in range(4):
                kp, par = h // 2, h % 2
                rs = slice(par * 64, (par + 1) * 64)
                K.MM(pO[:, h * 128:(h + 1) * 128], QKt[s][rs, kp, 0, :], Hbf[cur][rs, kp, :], True, False, [QKt[s], Hbf[cur]], [pO])
                K.MM(pO[:, h * 128:(h + 1) * 128], ST[s][:, h, :], Vt[s][:, h * 128:(h + 1) * 128], False, True, [ST[s], Vt[s]], [pO])
            K.CP("act", Ot[s][:], pO[:], [pO], [Ot[s]])
            K.DMA(D['O%d' % d][i * 128:(i + 1) * 128, :], Ot[s][:], [Ot[s]], [], q="pool")
            if KCUT <= 4:
                continue
            gv = pG[:].rearrange("p (a b) -> p a b", a=2)
            for kp in range(2):
                K.MM(gv[:, kp, :], KTM[s][:, kp, :], Vt[s][:, kp * 256:(kp + 1) * 256], True, True, [KTM[s], Vt[s]], [pG])
            nxt = 1 - cur
            for par in range(2):
                rs = slice(par * 64, (par + 1) * 64)
                pcb_ = PCt[rs, :, i:i + 1].to_broadcast([64, 2, 128])
                K.TT("dve", Ht[rs, :, :], H32[rs, :, :], gv[rs, :, par * 128:(par + 1) * 128], ALU.add, [H32, pG], [Ht])
                K.TT("dve", H32[rs, :, :], Ht[rs, :, :], pcb_, ALU.mult, [Ht, PCt], [H32])
                K.TT("dve", Hbf[nxt][rs, :, :], Ht[rs, :, :], pcb_, ALU.mult, [Ht, PCt], [Hbf[nxt]])
            cur = nxt
        K.P.emit()


def rms_res(K, srcs, rowt, roff, resid, outt, stt, eps=1e-6, n=1024):
    junk = stt['junk']; s4 = stt['s4']
    for j, (ap, bh) in enumerate(srcs):
        K.A(junk[:, 0:512], ap, AF.Square, [bh], [junk, s4], accum_out=s4[:, j:j + 1])
    K.TT("dve", s4[:, 2:3], s4[:, 0:1], s4[:, 1:2], ALU.add, [s4], [s4])
    K.A(s4[:, 3:4], s4[:, 2:3], AF.Sqrt, [s4], [s4], scale=1.0 / n, bias=eps)
    K.RECIP(s4[:, 4:5], s4[:, 3:4], [s4], [s4])
    for j, (ap, bh) in enumerate(srcs):
        cs = slice(j * 512, (j + 1) * 512)
        K.STT(junk[:, 512:1024], ap, s4[:, 4:5], rowt[:, roff + j * 512:roff + (j + 1) * 512], ALU.mult, ALU.mult, [bh, s4, rowt], [junk])
        K.TT("pool", outt[:, cs], junk[:, 512:1024], resid[:, cs], ALU.add, [junk, resid], [outt])


def load_cast_w(K, dst, src_ap, nk, ncols, stg, scale_cols=None, pv=None):
    step = stg[0].t.shape[1]
    i = 0
    for kc in range(nk):
        for c in range(0, ncols, step):
            n = min(step, ncols - c)
            s = stg[i % len(stg)]
            i += 1
            K.DMA(s[:, 0:n], src_ap[kc * 128:(kc + 1) * 128, c:c + n], [], [s], q="sp" if i % 2 else "act")
            if scale_cols is None:
                K.CP("pool" if i % 2 else "dve", dst[:, kc, c:c + n], s[:, 0:n], [s], [dst])
            else:
                K.TS("pool" if i % 2 else "dve", dst[:, kc, c:c + n], s[:, 0:n], pv[:, scale_cols + kc:scale_cols + kc + 1], None, ALU.mult, None, [s, pv], [dst])


def phase3a(nc, T, D):
    NT = T // 128
    with contextlib.ExitStack() as st:
        K = KB(nc, st)
        cst = K.sb([128, 128]); cstb = K.sb([128, 128], BF16)
        rows = K.sb([128, 2560])
        loraf = K.sb([128, 512]); gup = K.sb([128, 512], BF16)
        wout = K.sb([128, 8, 1024], BF16)
        stg = [K.sb([128, 1024]) for _ in range(3)]
        pb = [K.ps("pb%d" % i) for i in range(8)]
        K.DMA(cst[:], D['consts'][:, C_ID:C_ID + 128], [], [cst])
        K.CP("dve", cstb[:], cst[:], [cst], [cstb])
        K.DMA(rows[:], D['rows'][:, 0:2560], [], [rows])
        K.DMA(loraf[:], D['lora'][:, 1024:1536], [], [loraf])
        K.CP("dve", gup[:], loraf[:], [loraf], [gup])
        load_cast_w(K, wout, D['w_out'], 8, 1024, stg)
        L = {n_: [K.sb([128, 512]) for _ in range(2)] for n_ in ('Yf', 'Yb', 'BV', 'Of', 'Ob', 'GG')}
        SGt = [K.sb([128, 128], BF16) for _ in range(2)]
        xt = [K.sb([128, 1024]) for _ in range(2)]
        h1 = [K.sb([128, 1024]) for _ in range(2)]
        y = K.sb([128, 512]); sq = K.sb([128, 512]); o = K.sb([128, 512])
        s8 = K.sb([128, 64])
        ycat = K.sb([128, 1024], BF16); ycT = K.sb([128, 8, 128], BF16)
        stt = {'junk': K.sb([128, 1024]), 's4': K.sb([128, 8])}
        for i in range(NT):
            s = i % 2
            rsl = slice(i * 128, (i + 1) * 128)
            for j, (n_, key) in enumerate([('Yf', 'Y0'), ('Yb', 'Y1'), ('BV', 'BV_tm'), ('Of', 'O0'), ('Ob', 'O1'), ('GG', 'GG_tm')]):
                K.DMA(L[n_][s][:], D[key][rsl, :], [], [L[n_][s]], q="sp" if j % 2 else "pool")
            K.DMA(SGt[s][:], D['SG'][:, rsl], [], [SGt[s]])
            K.DMA(xt[s][:], D['x'][rsl, :], [], [xt[s]], q="pool")
            K.TT("pool", y[:], L['Yf'][s][:], L['Yb'][s][:], ALU.add, [L['Yf'][s], L['Yb'][s]], [y])
            y3 = y[:].rearrange("p (h c) -> p h c", h=8)
            K.RSUM(s8[:, 0:8], y3, [y], [s8])
            K.TT("pool", sq[:], y[:], y[:], ALU.mult, [y], [sq])
            K.RSUM(s8[:, 8:16], sq[:].rearrange("p (h c) -> p h c", h=8), [sq], [s8])
            K.TS("dve", s8[:, 16:24], s8[:, 0:8], 1.0 / 64, None, ALU.mult, None, [s8], [s8])
            K.TT("dve", s8[:, 24:32], s8[:, 16:24], s8[:, 16:24], ALU.mult, [s8], [s8])
            K.STT(s8[:, 32:40], s8[:, 8:16], 1.0 / 64, s8[:, 24:32], ALU.mult, ALU.subtract, [s8], [s8])
            K.A(s8[:, 40:48], s8[:, 32:40], AF.Sqrt, [s8], [s8], bias=64e-5)
            K.RECIP(s8[:, 48:56], s8[:, 40:48], [s8], [s8])
            K.TT("dve", y3, y3, s8[:, 16:24].unsqueeze(2).to_broadcast([128, 8, 64]), ALU.subtract, [y, s8], [y])
            K.TT("dve", y3, y3, s8[:, 48:56].unsqueeze(2).to_broadcast([128, 8, 64]), ALU.mult, [y, s8], [y])
            K.TT("pool", y[:], y[:], rows[:, 0:512], ALU.mult, [y, rows], [y])
            K.TT("pool", y[:], y[:], rows[:, 512:1024], ALU.add, [y, rows], [y])
            K.TT("pool", y[:], y[:], L['BV'][s][:], ALU.add, [y, L['BV'][s]], [y])
            K.MM(pb[0][:], SGt[s][:], gup[:], True, True, [SGt[s], gup], [pb[0]])
            K.TT("dve", ycat[:, 0:512], y[:], pb[0][:], ALU.mult, [y, pb[0]], [ycat])
            K.TT("pool", o[:], L['Of'][s][:], L['Ob'][s][:], ALU.add, [L['Of'][s], L['Ob'][s]], [o])
            o3 = o[:].rearrange("p (h c) -> p h c", h=4)
            K.TT("pool", sq[:], o[:], o[:], ALU.mult, [o], [sq])
            K.RSUM(s8[:, 56:60], sq[:].rearrange("p (h c) -> p h c", h=4), [sq], [s8])
            K.A(s8[:, 60:64], s8[:, 56:60], AF.Sqrt, [s8], [s8], scale=1.0 / 128, bias=1e-6)
            K.RECIP(s8[:, 56:60], s8[:, 60:64], [s8], [s8])
            K.TT("dve", o3, o3, s8[:, 56:60].unsqueeze(2).to_broadcast([128, 4, 128]), ALU.mult, [o, s8], [o])
            K.TT("pool", o[:], o[:], rows[:, 1024:1536], ALU.mult, [o, rows], [o])
            K.TT("dve", ycat[:, 512:1024], o[:], L['GG'][s][:], ALU.mult, [o, L['GG'][s]], [ycat])
            tv = pb[1][:].bitcast(BF16).rearrange("p (a b) -> p a b", a=8)
            for kc in range(8):
                K.TR(tv[:, kc, :], ycat[:, kc * 128:(kc + 1) * 128], cstb[:], [ycat, cstb], [pb[1]])
            K.CP("act", ycT[:], tv, [pb[1]], [ycT])
            bk = [pb[2 + 2 * s], pb[3 + 2 * s]]
            for nb in range(2):
                for kc in range(8):
                    K.MM(bk[nb][:], ycT[:, kc, :], wout[:, kc, nb * 512:(nb + 1) * 512], kc == 0, kc == 7, [ycT, wout], [bk[nb]])
            rms_res(K, [(bk[0][:], bk[0]), (bk[1][:], bk[1])], rows, 1536, xt[s], h1[s], stt)
            K.DMA(D['H1'][rsl, :], h1[s][:], [h1[s]], [])
        K.P.emit()


def phase3b(nc, T, D):
    NT = T // 128
    TS_ = 256 if T >= 256 else 128
    TPS = TS_ // 128
    with contextlib.ExitStack() as st:
        K = KB(nc, st)
        cst = K.sb([128, 128]); cstb = K.sb([128, 128], BF16)
        rows = K.sb([128, 1024]); pv = K.sb([128, NPV])
        wg = K.sb([128, 8, 2816], BF16); wu = K.sb([128, 8, 2816], BF16); wd = K.sb([128, 22, 1024], BF16)
        stg = [K.sb([128, 1408]) for _ in range(3)]
        pb = [K.ps("pb%d" % i) for i in range(8)]
        K.DMA(cst[:], D['consts'][:, C_ID:C_ID + 128], [], [cst])
        K.CP("dve", cstb[:], cst[:], [cst], [cstb])
        K.DMA(rows[:], D['rows'][:, R_NFP:R_NFP + 1024], [], [rows])
        K.DMA(pv[:], D['pvec'][:, :], [], [pv])
        load_cast_w(K, wg, D['ffn_gate'], 8, 2816, stg, NF_, pv)
        load_cast_w(K, wu, D['ffn_up'], 8, 2816, stg, NF_, pv)
        load_cast_w(K, wd, D['ffn_down'], 22, 1024, stg)
        h1 = [K.sb([128, 1024]) for _ in range(2 * TPS)]
        h2 = [K.sb([128, 1024]) for _ in range(2)]
        hnb = K.sb([128, 1024], BF16)
        hnT = K.sb([128, 8, TS_], BF16)
        act = K.sb([128, 22, TS_], BF16)
        sg = [K.sb([128, TS_]) for _ in range(2)]
        s4 = K.sb([128, 8]); junk = K.sb([128, 1024])
        stt = {'junk': junk, 's4': s4}
        for si in range(T // TS_):
            for j in range(TPS):
                i = si * TPS + j
                hb = h1[(si % 2) * TPS + j]
                K.DMA(hb[:], D['H1'][i * 128:(i + 1) * 128, :], [], [hb], q="sp" if j % 2 == 0 else "pool")
                K.A(junk[:], hb[:], AF.Square, [hb], [junk, s4], accum_out=s4[:, 5:6])
                K.A(s4[:, 6:7], s4[:, 5:6], AF.Sqrt, [s4], [s4], scale=1.0 / 1024, bias=1e-6)
                K.RECIP(s4[:, 7:8], s4[:, 6:7], [s4], [s4])
                K.TS("dve", hnb[:], hb[:], s4[:, 7:8], None, ALU.mult, None, [hb, s4], [hnb])
                tv = pb[0][:].bitcast(BF16).rearrange("p (a b) -> p a b", a=8)
                for kc in range(8):
                    K.TR(tv[:, kc, :], hnb[:, kc * 128:(kc + 1) * 128], cstb[:], [hnb, cstb], [pb[0]])
                K.CP("act", hnT[:, :, j * 128:(j + 1) * 128], tv, [pb[0]], [hnT])
            for fc in range(22):
                pg, pu = pb[1 + (fc % 2) * 2], pb[2 + (fc % 2) * 2]
                for kc in range(8):
                    K.MM(pg[:, 0:TS_], wg[:, kc, fc * 128:(fc + 1) * 128], hnT[:, kc, :], kc == 0, kc == 7, [wg, hnT], [pg])
                for kc in range(8):
                    K.MM(pu[:, 0:TS_], wu[:, kc, fc * 128:(fc + 1) * 128], hnT[:, kc, :], kc == 0, kc == 7, [wu, hnT], [pu])
                K.A(sg[fc % 2][:], pg[:, 0:TS_], AF.Silu, [pg], [sg[fc % 2]])
                K.TT("dve", act[:, fc, :], sg[fc % 2][:], pu[:, 0:TS_], ALU.mult, [sg[fc % 2], pu], [act])
            for j in range(TPS):
                i = si * TPS + j
                hb = h1[(si % 2) * TPS + j]
                bk = [pb[5], pb[6]]
                for nb in range(2):
                    for fc in range(22):
                        K.MM(bk[nb][:], act[:, fc, j * 128:(j + 1) * 128], wd[:, fc, nb * 512:(nb + 1) * 512], fc == 0, fc == 21, [act, wd], [bk[nb]])
                rms_res(K, [(bk[0][:], bk[0]), (bk[1][:], bk[1])], rows, 0, hb, h2[i % 2], stt)
                K.DMA(D['H2'][i * 128:(i + 1) * 128, :], h2[i % 2][:], [h2[i % 2]], [])
        K.P.emit()


def phase3c(nc, T, D):
    NT = T // 128
    with contextlib.ExitStack() as st:
        K = KB(nc, st)
        cst = K.sb([128, 128]); cstb = K.sb([128, 128], BF16)
        rows = K.sb([128, 2048])
        pproj = K.sb([128, 2, 1024], BF16); pgate = K.sb([128, 8, 1024], BF16)
        stg = [K.sb([128, 1024]) for _ in range(3)]
        pb = [K.ps("pb%d" % i) for i in range(8)]
        K.DMA(cst[:], D['consts'][:, C_ID:C_ID + 128], [], [cst])
        K.CP("dve", cstb[:], cst[:], [cst], [cstb])
        K.DMA(rows[:], D['rows'][:, R_NPL:R_NPL + 2048], [], [rows])
        load_cast_w(K, pproj, D['ple_proj'], 2, 1024, stg)
        load_cast_w(K, pgate, D['ple_gate'], 8, 1024, stg)
        h2 = [K.sb([128, 1024]) for _ in range(2)]
        pt = [K.sb([128, 256]) for _ in range(2)]
        ot = [K.sb([128, 1024]) for _ in range(2)]
        hb16 = K.sb([128, 1024], BF16); pb16 = K.sb([128, 256], BF16)
        hT = K.sb([128, 8, 128], BF16); pT = K.sb([128, 2, 128], BF16)
        ev = K.sb([128, 1024]); gt = K.sb([128, 1024])
        stt = {'junk': K.sb([128, 1024]), 's4': K.sb([128, 8])}
        for i in range(NT):
            s = i % 2
            rsl = slice(i * 128, (i + 1) * 128)
            K.DMA(h2[s][:], D['H2'][rsl, :], [], [h2[s]])
            K.DMA(pt[s][:], D['p'][rsl, :], [], [pt[s]], q="pool")
            K.CP("pool", hb16[:], h2[s][:], [h2[s]], [hb16])
            K.CP("pool", pb16[:], pt[s][:], [pt[s]], [pb16])
            tv = pb[0][:].bitcast(BF16).rearrange("p (a b) -> p a b", a=8)
            for kc in range(8):
                K.TR(tv[:, kc, :], hb16[:, kc * 128:(kc + 1) * 128], cstb[:], [hb16, cstb], [pb[0]])
            K.CP("act", hT[:], tv, [pb[0]], [hT])
            tv2 = pb[1][:].bitcast(BF16).rearrange("p (a b) -> p a b", a=8)
            for kc in range(2):
                K.TR(tv2[:, kc, :], pb16[:, kc * 128:(kc + 1) * 128], cstb[:], [pb16, cstb], [pb[1]])
            K.CP("act", pT[:], tv2[:, 0:2, :], [pb[1]], [pT])
            for nb in range(2):
                cs = slice(nb * 512, (nb + 1) * 512)
                pe_, pg_ = pb[2 + nb], pb[4 + nb]
                for kc in range(2):
                    K.MM(pe_[:], pT[:, kc, :], pproj[:, kc, cs], kc == 0, kc == 1, [pT, pproj], [pe_])
                for kc in range(8):
                    K.MM(pg_[:], hT[:, kc, :], pgate[:, kc, cs], kc == 0, kc == 7, [hT, pgate], [pg_])
                K.TT("dve", gt[:, cs], pg_[:], rows[:, 1024 + nb * 512:1024 + (nb + 1) * 512], ALU.add, [pg_, rows], [gt])
                K.A(gt[:, cs], gt[:, cs], AF.Sigmoid, [gt], [gt])
                K.TT("dve", ev[:, cs], gt[:, cs], pe_[:], ALU.mult, [gt, pe_], [ev])
            rms_res(K, [(ev[:, 0:512], ev), (ev[:, 512:1024], ev)], rows, 0, h2[s], ot[s], stt)
            K.DMA(D['out'][rsl, :], ot[s][:], [ot[s]], [])
        K.P.emit()


_KEEP = []


def build(T, debug=False):
    NT = T // 128
    TB = min(512, T)
    nc = bass.Bass("TRN2", target_bir_lowering=False)
    D = {}

    def din(name, shape):
        D[name] = nc.dram_tensor(name, shape, F32, kind="ExternalInput").ap()
    din('x', [T, 1024]); din('p', [T, 256]); din('w_in_p', [1024, NG * 128]); din('pvec', [128, NPV])
    din('rows', [128, NROW]); din('lora', [128, 1792]); din('consts', [128, NCST]); din('resets', [128, 2 * TB])
    din('w_out', [1024, 1024]); din('ffn_gate', [1024, 2816]); din('ffn_up', [1024, 2816]); din('ffn_down', [2816, 1024])
    din('ple_proj', [256, 1024]); din('ple_gate', [1024, 1024])
    D['out'] = nc.dram_tensor("out", [T, 1024], F32, kind="ExternalOutput").ap()
    kind = "ExternalOutput" if debug else "Internal"

    def scr(name, shape, dt):
        D[name] = nc.dram_tensor(name, shape, dt, kind=kind).ap()
    for d in range(2):
        scr('RK%d' % d, [NT, 128, 4, 2, 128], BF16); scr('KB%d' % d, [NT, 128, 4, 2, 128], BF16)
        scr('PC%d' % d, [128, 4, T // 64], F32)
        scr('QK%d' % d, [NT, 128, 2, 2, 128], BF16); scr('PCG%d' % d, [128, 2, NT], F32)
        scr('Y%d' % d, [T, 512], F32); scr('O%d' % d, [T, 512], F32)
    scr('V_tm', [T, 512], BF16); scr('VG_tm', [T, 512], BF16)
    scr('BV_tm', [T, 512], F32); scr('GG_tm', [T, 512], F32)
    scr('SG', [128, T], BF16)
    scr('H1', [T, 1024], F32); scr('H2', [T, 1024], F32)
    import os
    lim = int(os.environ.get("KPH", "9"))
    _st = contextlib.ExitStack()
    Prog.init_sems(nc, _st)
    nc._keep_st = _st if False else None
    _KEEP.append(_st)
    only = os.environ.get('KONLY')
    if only == 'r0':
        phase2_rwkv(nc, T, D, 0)
        return nc
    if only == 'g0':
        phase2_gla(nc, T, D, 0)
        return nc
    if only == '1g':
        phase1(nc, T, D)
        phase2_gla(nc, T, D, 0)
        return nc
    if only == '1r':
        phase1(nc, T, D)
        phase2_rwkv(nc, T, D, 0)
        return nc
    if only == '3a':
        phase3a(nc, T, D)
        return nc
    phase1(nc, T, D)
    if lim >= 2:
        for d in range(2):
            phase2_rwkv(nc, T, D, d)
    if lim >= 3:
        for d in range(2):
            phase2_gla(nc, T, D, d)
    if lim >= 4:
        phase3a(nc, T, D)
    if lim >= 5:
        phase3b(nc, T, D)
    if lim >= 6:
        phase3c(nc, T, D)
    return nc


_CACHE = {}


def kernel(**inputs):
    I = {k: np.asarray(v) for k, v in inputs.items()}
    B, T = I['x'].shape[0], I['x'].shape[1]
    hp = host_prep(I)
    consts, resets = make_consts(min(512, T))
    if T not in _CACHE:
        _CACHE[T] = build(T)
    nc = _CACHE[T]
    in_maps = []
    for b in range(B):
        m = dict(hp)
        m['x'] = np.ascontiguousarray(I['x'][b]); m['p'] = np.ascontiguousarray(I['p'][0, b])
        m['consts'] = consts; m['resets'] = resets
        in_maps.append(m)
    res = run_bass_kernel_spmd(nc, in_maps, core_ids=list(range(B)))
    return np.stack([r['out'] for r in res.results], 0).astype(np.float32)
```
